# Optimizing a Trainium2 kernel written in Bass

```python
import jax, jax.numpy as jnp
from jax import lax
import numpy as np

D_MODEL = 1024
BATCH = 8
SEQ = 2048
DEPTH = 2

GRID_W = 64
CTX_LEN = 256
N_EVEN = (DEPTH + 1) // 2
N_ODD = DEPTH // 2

MLA_HEADS = 8
MLA_Q_RANK = 384
MLA_KV_RANK = 256
MLA_NOPE = 64
MLA_ROPE = 32
MLA_V = 64
MLA_QK = MLA_NOPE + MLA_ROPE
GQA_HEADS = 8
GQA_KV_HEADS = 2
GQA_DIM = 64
EVEN_SPLITS = (MLA_Q_RANK, MLA_KV_RANK, MLA_ROPE, GQA_HEADS * GQA_DIM,
               GQA_KV_HEADS * GQA_DIM, GQA_KV_HEADS * GQA_DIM)
EVEN_IN = sum(EVEN_SPLITS)
EVEN_MIX = MLA_HEADS * MLA_V + GQA_HEADS * GQA_DIM
MLSTM_HEADS = 8
MLSTM_QK = 64
MLSTM_V = 128
MLSTM_CONV = 3
MLSTM_CHUNK = 64
ODD_SPLITS = (2 * MLSTM_HEADS * MLSTM_QK, MLSTM_HEADS * MLSTM_V, MLSTM_HEADS * MLSTM_V, 4 * MLSTM_HEADS)
ODD_IN = sum(ODD_SPLITS)
D_FF = 2816
FFN_CONV = 3

Q_BLOCK = 128
ROPE_THETA = 10000.0
EPS = 1e-6

kernel_name = "hybrid_mla_gqa_mlstm_convffn_dit"


def rmsnorm(x, g):
    xf = x.astype(jnp.float32)
    y = xf * lax.rsqrt(jnp.mean(xf * xf, axis=-1, keepdims=True) + EPS)
    return (y * g.astype(jnp.float32)).astype(x.dtype)


def modulate(h, shift, scale):
    return h * (1 + scale) + shift


def split_cols(p, sizes):
    return jnp.split(p, np.cumsum(sizes)[:-1].tolist(), axis=-1)


def dwconv(x, w, b):
    width, T = w.shape[0], x.shape[1]
    left = (width - 1) // 2
    xp = jnp.pad(x, ((0, 0), (left, width - 1 - left), (0, 0)))
    return sum(xp[:, j:j + T] * w[j] for j in range(width)) + b


def axial_rope_tables(n_tokens, rot_dim):
    rows = n_tokens // GRID_W
    row = jnp.repeat(jnp.arange(rows), GRID_W).astype(jnp.float32)
    col = jnp.tile(jnp.arange(GRID_W), rows).astype(jnp.float32)
    n_freq = rot_dim // 4
    inv = ROPE_THETA ** (-jnp.arange(n_freq, dtype=jnp.float32) / n_freq)
    a_r, a_c = row[:, None] * inv, col[:, None] * inv
    ang = jnp.concatenate([a_r, a_r, a_c, a_c], axis=-1)
    return jnp.cos(ang), jnp.sin(ang)


def apply_rope(x, cos, sin):
    a1, a2, b1, b2 = jnp.split(x, 4, axis=-1)
    rot = jnp.concatenate([-a2, a1, -b2, b1], axis=-1)
    shape = (1, x.shape[1]) + (1,) * (x.ndim - 3) + (x.shape[-1],)
    return x * cos.reshape(shape).astype(x.dtype) + rot * sin.reshape(shape).astype(x.dtype)


def attend(q, k, v):
    B, T, G, R, dk = q.shape
    nb = T // Q_BLOCK
    scale = dk ** -0.5
    qb = jnp.moveaxis(q.reshape(B, nb, Q_BLOCK, G, R, dk), 1, 0)

    def block(qi):
        s = jnp.einsum('bqgrd,bsgd->bgrqs', qi, k).astype(jnp.float32) * scale
        p = jax.nn.softmax(s, axis=-1).astype(v.dtype)
        return jnp.einsum('bgrqs,bsgd->bqgrd', p, v)

    out = lax.map(block, qb)
    return jnp.moveaxis(out, 0, 1).reshape(B, T, G, R, v.shape[-1])


def mla_q(cq, g_qa, w_qb, g_q, rope):
    B, T, _ = cq.shape
    q = jnp.dot(rmsnorm(cq, g_qa), w_qb).reshape(B, T, MLA_HEADS, MLA_QK)
    q = rmsnorm(q, g_q)
    if rope is not None:
        q = jnp.concatenate([q[..., :MLA_NOPE], apply_rope(q[..., MLA_NOPE:], *rope)], axis=-1)
    return q[:, :, :, None, :]


def mla_kv(ckv, k_rope, g_kva, w_kvb, g_k, rope):
    B, T, _ = ckv.shape
    kv = jnp.dot(rmsnorm(ckv, g_kva), w_kvb).reshape(B, T, MLA_HEADS, MLA_NOPE + MLA_V)
    k_nope, v = kv[..., :MLA_NOPE], kv[..., MLA_NOPE:]
    k_pe = jnp.broadcast_to(k_rope[:, :, None, :], (B, T, MLA_HEADS, MLA_ROPE))
    k = rmsnorm(jnp.concatenate([k_nope, k_pe], axis=-1), g_k)
    if rope is not None:
        k = jnp.concatenate([k[..., :MLA_NOPE], apply_rope(k[..., MLA_NOPE:], *rope)], axis=-1)
    return k, v


def gqa_q(q, g_q, rope):
    B, T, _ = q.shape
    q = rmsnorm(q.reshape(B, T, GQA_HEADS, GQA_DIM), g_q)
    if rope is not None:
        q = apply_rope(q, *rope)
    return q.reshape(B, T, GQA_KV_HEADS, GQA_HEADS // GQA_KV_HEADS, GQA_DIM)


def gqa_kv(k, v, g_k, rope):
    B, T, _ = k.shape
    k = rmsnorm(k.reshape(B, T, GQA_KV_HEADS, GQA_DIM), g_k)
    if rope is not None:
        k = apply_rope(k, *rope)
    return k, v.reshape(B, T, GQA_KV_HEADS, GQA_DIM)


def even_mixer(h_lat, h_ctx, w_in, g_qa, w_qb, g_kva, w_kvb, g_mq, g_mk, g_gq, g_gk, w_out,
               rope_mla, rope_gqa, need_ctx_out):
    cq_l, ckv_l, kr_l, gq_l, gk_l, gv_l = split_cols(jnp.dot(h_lat, w_in), EVEN_SPLITS)
    cq_c, ckv_c, kr_c, gq_c, gk_c, gv_c = split_cols(jnp.dot(h_ctx, w_in), EVEN_SPLITS)
    ka_c, va_c = mla_kv(ckv_c, kr_c, g_kva, w_kvb, g_mk, None)
    ka_l, va_l = mla_kv(ckv_l, kr_l, g_kva, w_kvb, g_mk, rope_mla)
    kb_c, vb_c = gqa_kv(gk_c, gv_c, g_gk, None)
    kb_l, vb_l = gqa_kv(gk_l, gv_l, g_gk, rope_gqa)
    ka, va = jnp.concatenate([ka_c, ka_l], axis=1), jnp.concatenate([va_c, va_l], axis=1)
    kb, vb = jnp.concatenate([kb_c, kb_l], axis=1), jnp.concatenate([vb_c, vb_l], axis=1)

    def merge(o_a, o_b):
        B, T = o_a.shape[:2]
        return jnp.dot(jnp.concatenate([o_a.reshape(B, T, -1), o_b.reshape(B, T, -1)], axis=-1), w_out)

    out_lat = merge(attend(mla_q(cq_l, g_qa, w_qb, g_mq, rope_mla), ka, va),
                    attend(gqa_q(gq_l, g_gq, rope_gqa), kb, vb))
    out_ctx = None
    if need_ctx_out:
        out_ctx = merge(attend(mla_q(cq_c, g_qa, w_qb, g_mq, None), ka_c, va_c),
                        attend(gqa_q(gq_c, g_gq, None), kb_c, vb_c))
    return out_lat, out_ctx


def mlstm_inputs(h, w_in, conv_w, conv_b, gate_b):
    B, T, _ = h.shape
    qk, v, o, g = split_cols(jnp.dot(h, w_in), ODD_SPLITS)
    q, k = jnp.split(jax.nn.silu(dwconv(qk, conv_w, conv_b)), 2, axis=-1)

    def heads(t, d):
        return jnp.swapaxes(t.reshape(B, T, MLSTM_HEADS, d), 1, 2).astype(jnp.float32)

    g = (g + gate_b).astype(jnp.float32).reshape(B, T, 4, MLSTM_HEADS)
    g = jnp.transpose(g, (2, 0, 3, 1))
    gates = (g[0], jax.nn.log_sigmoid(g[1]), g[2], jax.nn.log_sigmoid(g[3]))
    return heads(q, MLSTM_QK), heads(k, MLSTM_QK) * MLSTM_QK ** -0.5, heads(v, MLSTM_V), o, gates


def mlstm_state_update(state, k, v, log_i, b):
    C, n, m = state
    b_last = b[..., -1]
    a = b_last[..., None] - b + log_i
    m_new = jnp.maximum(b_last + m, jnp.max(a, axis=-1))
    decay = jnp.exp(b_last + m - m_new)
    w = jnp.exp(a - m_new[..., None])
    C_new = decay[..., None, None] * C + jnp.einsum('bhs,bhsv,bhsd->bhvd', w, v, k)
    n_new = decay[..., None] * n + jnp.einsum('bhs,bhsd->bhd', w, k)
    return (C_new, n_new, m_new)


def mlstm_chunkwise(q, k, v, log_i, log_f, state0):
    B, H, T, dk = q.shape
    dv = v.shape[-1]
    L = MLSTM_CHUNK
    nc = T // L
    lower = jnp.tril(jnp.ones((L, L), dtype=bool))

    def chunks(t):
        return jnp.moveaxis(t.reshape((B, H, nc, L) + t.shape[3:]), 2, 0)

    def step(state, inp):
        C, n, m = state
        qc, kc, vc, ic, fc = inp
        b = jnp.cumsum(fc, axis=-1)
        logw = jnp.where(lower, b[..., :, None] - b[..., None, :] + ic[..., None, :], -jnp.inf)
        log_inter = b + m[..., None]
        m_t = jnp.maximum(log_inter, jnp.max(logw, axis=-1))
        w_inter = jnp.exp(log_inter - m_t)
        s = jnp.einsum('bhtd,bhsd->bhts', qc, kc) * jnp.exp(logw - m_t[..., None])
        num = w_inter[..., None] * jnp.einsum('bhvd,bhtd->bhtv', C, qc) + jnp.einsum('bhts,bhsv->bhtv', s, vc)
        den = w_inter * jnp.einsum('bhd,bhtd->bht', n, qc) + jnp.sum(s, axis=-1)
        h = num / jnp.maximum(jnp.abs(den), jnp.exp(-m_t))[..., None]
        return mlstm_state_update(state, kc, vc, ic, b), h

    state, h = lax.scan(step, state0, (chunks(q), chunks(k), chunks(v), chunks(log_i), chunks(log_f)))
    return jnp.moveaxis(h, 0, 2).reshape(B, H, T, dv), state


def mlstm_direction(q_l, k_l, v_l, li_l, lf_l, q_c, k_c, v_c, li_c, lf_c, need_ctx_out):
    B, H = q_l.shape[:2]
    state0 = (jnp.zeros((B, H, MLSTM_V, MLSTM_QK), jnp.float32),
              jnp.zeros((B, H, MLSTM_QK), jnp.float32),
              jnp.zeros((B, H), jnp.float32))
    if need_ctx_out:
        h_c, state_c = mlstm_chunkwise(q_c, k_c, v_c, li_c, lf_c, state0)
    else:
        h_c = None
        state_c = mlstm_state_update(state0, k_c, v_c, li_c, jnp.cumsum(lf_c, axis=-1))
    h_l, _ = mlstm_chunkwise(q_l, k_l, v_l, li_l, lf_l, state_c)
    return h_l, h_c


def odd_mixer(h_lat, h_ctx, w_in, conv_w, conv_b, gate_b, out_g, w_out, need_ctx_out):
    q_l, k_l, v_l, o_l, g_l = mlstm_inputs(h_lat, w_in, conv_w, conv_b, gate_b)
    q_c, k_c, v_c, o_c, g_c = mlstm_inputs(h_ctx, w_in, conv_w, conv_b, gate_b)

    def rev(t):
        return jnp.flip(t, axis=2)

    hf_l, hf_c = mlstm_direction(q_l, k_l, v_l, g_l[0], g_l[1], q_c, k_c, v_c, g_c[0], g_c[1], need_ctx_out)
    hb_l, hb_c = mlstm_direction(rev(q_l), rev(k_l), rev(v_l), rev(g_l[2]), rev(g_l[3]),
                                 rev(q_c), rev(k_c), rev(v_c), rev(g_c[2]), rev(g_c[3]), need_ctx_out)

    def readout(hf, hb, o):
        h = jnp.swapaxes(hf + rev(hb), 1, 2)
        B, T = h.shape[:2]
        h = rmsnorm(h, out_g).reshape(B, T, -1).astype(o.dtype)
        return jnp.dot(h * jax.nn.sigmoid(o), w_out)

    out_lat = readout(hf_l, hb_l, o_l)
    out_ctx = readout(hf_c, hb_c, o_c) if need_ctx_out else None
    return out_lat, out_ctx


def conv_ffn(h, w_up, conv_w, conv_b, w_down):
    gate, val = jnp.split(jnp.dot(h, w_up), 2, axis=-1)
    return jnp.dot(jax.nn.silu(dwconv(gate, conv_w, conv_b)) * val, w_down)


def setup_inputs(seed: int = 0) -> dict:
    key = jax.random.key(seed)
    keys = jax.random.split(key, 40)
    counter = iter(range(40))
    D = D_MODEL

    def nrm(shape, scale):
        return jax.random.normal(keys[next(counter)], shape, jnp.float32) * scale

    def gain(shape):
        return 1.0 + nrm(shape, 0.05)

    H = MLSTM_HEADS
    i_bias = nrm((N_ODD, 2, 1, H), 0.1)
    f_bias = jnp.linspace(3.0, 6.0, H, dtype=jnp.float32) + nrm((N_ODD, 2, 1, H), 0.1)
    ml_gate_b = jnp.concatenate([i_bias, f_bias], axis=2).reshape(N_ODD, 4 * H)
    return {
        'x': nrm((BATCH, SEQ, D), 1.0),
        'c': nrm((BATCH, D), 1.0),
        'ctx': nrm((BATCH, CTX_LEN, D), 1.0),
        'c_ctx': nrm((D,), 1.0),
        'ada_w': nrm((DEPTH, D, 6 * D), 0.5 * D ** -0.5),
        'ada_b': nrm((DEPTH, 6 * D), 0.02),
        'norm1_g': gain((DEPTH, D)),
        'norm2_g': gain((DEPTH, D)),
        'ffn_w_up': nrm((DEPTH, D, 2 * D_FF), D ** -0.5),
        'ffn_conv_w': nrm((DEPTH, FFN_CONV, D_FF), 0.5),
        'ffn_conv_b': nrm((DEPTH, D_FF), 0.02),
        'ffn_w_down': nrm((DEPTH, D_FF, D), D_FF ** -0.5),
        'att_w_in': nrm((N_EVEN, D, EVEN_IN), D ** -0.5),
        'mla_qa_g': gain((N_EVEN, MLA_Q_RANK)),
        'mla_w_qb': nrm((N_EVEN, MLA_Q_RANK, MLA_HEADS * MLA_QK), MLA_Q_RANK ** -0.5),
        'mla_kva_g': gain((N_EVEN, MLA_KV_RANK)),
        'mla_w_kvb': nrm((N_EVEN, MLA_KV_RANK, MLA_HEADS * (MLA_NOPE + MLA_V)), MLA_KV_RANK ** -0.5),
        'mla_q_g': gain((N_EVEN, MLA_QK)),
        'mla_k_g': gain((N_EVEN, MLA_QK)),
        'gqa_q_g': gain((N_EVEN, GQA_DIM)),
        'gqa_k_g': gain((N_EVEN, GQA_DIM)),
        'att_w_out': nrm((N_EVEN, EVEN_MIX, D), EVEN_MIX ** -0.5),
        'ml_w_in': nrm((N_ODD, D, ODD_IN), D ** -0.5),
        'ml_conv_w': nrm((N_ODD, MLSTM_CONV, 2 * H * MLSTM_QK), 0.5),
        'ml_conv_b': nrm((N_ODD, 2 * H * MLSTM_QK), 0.02),
        'ml_gate_b': ml_gate_b,
        'ml_out_g': gain((N_ODD, H, MLSTM_V)),
        'ml_w_out': nrm((N_ODD, H * MLSTM_V, D), (H * MLSTM_V) ** -0.5),
    }


def reference(x, c, ctx, c_ctx, ada_w, ada_b, norm1_g, norm2_g, ffn_w_up, ffn_conv_w, ffn_conv_b,
              ffn_w_down, att_w_in, mla_qa_g, mla_w_qb, mla_kva_g, mla_w_kvb, mla_q_g, mla_k_g,
              gqa_q_g, gqa_k_g, att_w_out, ml_w_in, ml_conv_w, ml_conv_b, ml_gate_b, ml_out_g, ml_w_out):
    n_lat = x.shape[1]
    rope_mla = axial_rope_tables(n_lat, MLA_ROPE)
    rope_gqa = axial_rope_tables(n_lat, GQA_DIM)
    for layer in range(DEPTH):
        need_ctx_out = layer < DEPTH - 1
        j = layer // 2
        mod_lat = (jnp.dot(jax.nn.silu(c), ada_w[layer]) + ada_b[layer])[:, None, :]
        mod_ctx = jnp.dot(jax.nn.silu(c_ctx), ada_w[layer]) + ada_b[layer]
        sh1, sc1, gt1, sh2, sc2, gt2 = jnp.split(mod_lat, 6, axis=-1)
        csh1, csc1, cgt1, csh2, csc2, cgt2 = jnp.split(mod_ctx, 6, axis=-1)
        h_lat = modulate(rmsnorm(x, norm1_g[layer]), sh1, sc1)
        h_ctx = modulate(rmsnorm(ctx, norm1_g[layer]), csh1, csc1)
        if layer % 2 == 0:
            out_lat, out_ctx = even_mixer(h_lat, h_ctx, att_w_in[j], mla_qa_g[j], mla_w_qb[j], mla_kva_g[j],
                                          mla_w_kvb[j], mla_q_g[j], mla_k_g[j], gqa_q_g[j], gqa_k_g[j],
                                          att_w_out[j], rope_mla, rope_gqa, need_ctx_out)
        else:
            out_lat, out_ctx = odd_mixer(h_lat, h_ctx, ml_w_in[j], ml_conv_w[j], ml_conv_b[j], ml_gate_b[j],
                                         ml_out_g[j], ml_w_out[j], need_ctx_out)
        x = x + gt1 * out_lat
        x = x + gt2 * conv_ffn(modulate(rmsnorm(x, norm2_g[layer]), sh2, sc2),
                               ffn_w_up[layer], ffn_conv_w[layer], ffn_conv_b[layer], ffn_w_down[layer])
        if need_ctx_out:
            ctx = ctx + cgt1 * out_ctx
            ctx = ctx + cgt2 * conv_ffn(modulate(rmsnorm(ctx, norm2_g[layer]), csh2, csc2),
                                       ffn_w_up[layer], ffn_conv_w[layer], ffn_conv_b[layer], ffn_w_down[layer])
    return x
```

```python
import contextlib
import math
import numpy as np
import concourse.bass as bass
import concourse.mybir as mybir
from concourse.bass_utils import run_bass_kernel_spmd

F32 = mybir.dt.float32
BF16 = mybir.dt.bfloat16
ALU = mybir.AluOpType
AF = mybir.ActivationFunctionType
AX = mybir.AxisListType

N_DMA_SEMS = 16
D = 1024
TC = 256
TL = 2048
T = TC + TL
KC = 8
DFF = 2816
NF = 22
EPS = 1e-6
GRID_W = 64


class Sched:
    ENGS = ['pe', 'dve', 'act', 'pool', 'sp']

    def __init__(self, nc):
        self.nc = nc
        self.ops = []
        self.lastw = {}
        self.readers = {}
        self.dma_lists = {'hw': [], 'sw': []}
        self.last_on = {}
        self.dma_since_fence = []

    def add(self, eng, fn, r=(), w=(), dma=False, extra=()):
        idx = len(self.ops)
        deps = set(extra)
        for k in r:
            if k in self.lastw:
                deps.add(self.lastw[k])
        for k in w:
            if k in self.lastw:
                deps.add(self.lastw[k])
            deps.update(self.readers.get(k, ()))
        for k in r:
            self.readers.setdefault(k, []).append(idx)
        for k in w:
            self.lastw[k] = idx
            self.readers[k] = []
        deps.discard(idx)
        dma_kind = None
        dma_i = None
        if dma:
            dma_kind = 'sw' if eng == 'pool' else 'hw'
            lst = self.dma_lists[dma_kind]
            n = len(lst)
            if n >= N_DMA_SEMS:
                deps.add(lst[n - N_DMA_SEMS])
            lst.append(idx)
            dma_i = n
            self.dma_since_fence.append(idx)
        self.ops.append(dict(eng=eng, fn=fn, deps=deps, dma=dma, dma_i=dma_i, dma_kind=dma_kind))
        self.last_on[eng] = idx
        return idx

    def pe(self, fn, r=(), w=()):
        return self.add('pe', fn, r, w)

    def dve(self, fn, r=(), w=()):
        return self.add('dve', fn, r, w)

    def act(self, fn, r=(), w=()):
        return self.add('act', fn, r, w)

    def pool(self, fn, r=(), w=()):
        return self.add('pool', fn, r, w)

    def dma(self, eng, fn, r=(), w=()):
        return self.add(eng, fn, r, w, dma=True)

    def fence(self):
        base = set(self.last_on.values()) | set(self.dma_since_fence)
        self.dma_since_fence = []
        for e in self.ENGS:
            self.add(e, lambda eng: eng.nop(), extra=base)

    def emit(self):
        nc = self.nc
        ops = self.ops
        engs = self.ENGS
        pos = {}
        streams = {e: [] for e in engs}
        for i, o in enumerate(ops):
            pos[i] = len(streams[o['eng']])
            streams[o['eng']].append(i)
        waited = {e: {p: -1 for p in engs} for e in engs}
        waited_dma = {e: set() for e in engs}
        signalling = set()
        waits = {}
        for i, o in enumerate(ops):
            e = o['eng']
            wl = []
            best = {}
            for d in sorted(o['deps']):
                od = ops[d]
                if od['dma']:
                    if d not in waited_dma[e]:
                        waited_dma[e].add(d)
                        wl.append(('dma', d))
                else:
                    ep = od['eng']
                    if ep == 'pe' and e == 'pe':
                        continue
                    if waited[e][ep] >= pos[d]:
                        continue
                    if ep not in best or pos[d] > pos[best[ep]]:
                        best[ep] = d
            for ep, d in best.items():
                waited[e][ep] = pos[d]
                signalling.add(d)
                wl.append(('eng', d))
            waits[i] = wl
        rank = {}
        cnt = {e: 0 for e in engs}
        for i, o in enumerate(ops):
            if i in signalling:
                cnt[o['eng']] += 1
                rank[i] = cnt[o['eng']]
        self.stats = dict(n_ops=len(ops), n_sig=len(signalling), per_eng={e: len(streams[e]) for e in engs})
        with contextlib.ExitStack() as st:
            sem = {e: st.enter_context(nc.semaphore('s_' + e)) for e in engs}
            dsem = {kind: [st.enter_context(nc.semaphore('d%s_%d' % (kind, j))) for j in range(N_DMA_SEMS)] for kind in ('hw', 'sw')}
            block = st.enter_context(nc.Block())

            def run_stream(e, engine):
                for i in streams[e]:
                    o = ops[i]
                    for kind, d in waits[i]:
                        if kind == 'dma':
                            di = ops[d]['dma_i']
                            engine.wait_ge(dsem[ops[d]['dma_kind']][di % N_DMA_SEMS], 16 * (di // N_DMA_SEMS + 1))
                        else:
                            engine.wait_ge(sem[ops[d]['eng']], rank[d])
                    ins = o['fn'](engine)
                    if o['dma']:
                        di = o['dma_i']
                        ins.then_inc(dsem[o['dma_kind']][di % N_DMA_SEMS], 16)
                    elif i in signalling:
                        ins.then_inc(sem[e], 1)

            @block.tensor
            def _(eng):
                run_stream('pe', eng)

            @block.vector
            def _(eng):
                run_stream('dve', eng)

            @block.scalar
            def _(eng):
                run_stream('act', eng)

            @block.gpsimd
            def _(eng):
                run_stream('pool', eng)

            @block.sync
            def _(eng):
                run_stream('sp', eng)


def _pk(w):
    kk = w.shape[0] // 128
    return np.ascontiguousarray(w.reshape(kk, 128, -1).transpose(1, 0, 2))


def _colvec(v, parts=128):
    kk = v.shape[0] // parts
    out = np.zeros((128, kk), np.float32)
    out[:parts, :] = v.reshape(kk, parts).T
    return out


def _rope_tables():
    def tab(rot_dim):
        rows = TL // GRID_W
        row = np.repeat(np.arange(rows), GRID_W).astype(np.float32)
        col = np.tile(np.arange(GRID_W), rows).astype(np.float32)
        n_freq = rot_dim // 4
        inv = (np.float32(10000.0) ** (-np.arange(n_freq, dtype=np.float32) / np.float32(n_freq))).astype(np.float32)
        a_r, a_c = row[:, None] * inv, col[:, None] * inv
        ang = np.concatenate([a_r, a_r, a_c, a_c], axis=-1).astype(np.float32)
        return np.cos(ang).astype(np.float32).T, np.sin(ang).astype(np.float32).T
    cg, sg = tab(64)
    cm, sm = tab(32)
    cosT = np.zeros((128, TL), np.float32)
    sinT = np.zeros((128, TL), np.float32)
    cosT[0:64], sinT[0:64] = cg, sg
    cosT[64:96], sinT[64:96] = cm, sm
    return cosT, sinT


def _rot_mat(dk, r0, r):
    m = np.zeros((128, 128), np.float32)
    q4 = r // 4
    for i in range(r):
        p = r0 + i
        quarter = i // q4
        if quarter in (0, 2):
            m[p + q4, p] = -1.0
        else:
            m[p - q4, p] = 1.0
    return m


class VT:
    pass


def _build_vecs(inp):
    cols = []
    off = {}

    def put(name, arr):
        off[name] = sum(c.shape[1] for c in cols)
        cols.append(np.ascontiguousarray(arr, dtype=np.float32))

    for l in range(2):
        put('n1g%d' % l, _colvec(inp['norm1_g'][l]))
        put('n2g%d' % l, _colvec(inp['norm2_g'][l]))
        put('adab%d' % l, _colvec(inp['ada_b'][l]))
        cw = inp['ffn_conv_w'][l]
        for j in range(3):
            put('fcw%d_%d' % (l, j), _colvec(cw[j]))
        put('fcb%d' % l, _colvec(inp['ffn_conv_b'][l]))
    put('qag', _colvec(inp['mla_qa_g'][0]))
    put('kvag', _colvec(inp['mla_kva_g'][0]))
    put('mqg', _colvec(inp['mla_q_g'][0], 96))
    put('mkg', _colvec(inp['mla_k_g'][0], 96))
    put('gqg', _colvec(inp['gqa_q_g'][0], 64))
    put('gkg', _colvec(inp['gqa_k_g'][0], 64))
    mcw = inp['ml_conv_w'][0]
    for j in range(3):
        put('mcw%d' % j, _colvec(mcw[j], 64))
    put('mcb', _colvec(inp['ml_conv_b'][0], 64))
    put('mog', _colvec(inp['ml_out_g'][0].reshape(-1)))
    for j in range(3):
        put('mqk_w%d' % j, np.concatenate([mcw[j][0:512].reshape(8, 64).T, mcw[j][512:1024].reshape(8, 64).T], axis=0))
    mcb_ = inp['ml_conv_b'][0]
    put('mqk_b', np.concatenate([mcb_[0:512].reshape(8, 64).T, mcb_[512:1024].reshape(8, 64).T], axis=0))
    put('mqk_sc', np.concatenate([np.full((64, 1), 1.0, np.float32), np.full((64, 1), 0.125, np.float32)], axis=0))
    gb = inp['ml_gate_b'][0]
    put('gateb', np.broadcast_to(gb[None, :], (128, 32)))
    vec = np.concatenate(cols, axis=1)
    return vec, off


def _consts():
    ident = np.eye(128, dtype=np.float32)
    u = np.arange(128)
    tri_le = (u[:, None] <= u[None, :]).astype(np.float32)
    tri_ge = (u[:, None] >= u[None, :]).astype(np.float32)
    tri_lt = (u[:, None] < u[None, :]).astype(np.float32)
    tri_gt = (u[:, None] > u[None, :]).astype(np.float32)
    psel = np.zeros((128, 128), np.float32)
    for j in range(32):
        psel[j, 64 + j] = 1.0
    rg = _rot_mat(64, 0, 64)
    rm = _rot_mat(96, 64, 32)
    return np.stack([ident, tri_le, tri_ge, tri_lt, tri_gt, psel, rg, rm], axis=1)


C_IDENT, C_LE, C_GE, C_LT, C_GT, C_PSEL, C_RG, C_RM = range(8)

ARENA_BYTES = 206848


class StopEmit(Exception):
    pass
W_CQ, W_CKV, W_KR, W_GQ, W_GK, W_GV = 0, 384, 640, 672, 1184, 1312


def build_program(voff, nv, flags):
    nc = bass.Bass("TRN2", target_bir_lowering=False)

    def din(name, shape):
        return nc.dram_tensor(name, list(shape), F32, kind="ExternalInput").ap()

    xT_in = din("xT_in", [128, KC, T])
    c_in = din("c_in", [128, KC, 2])
    vecs_in = din("vecs", [128, nv])
    consts_in = din("consts", [128, 8, 128])
    cos_in = din("cosT", [128, TL])
    sin_in = din("sinT", [128, TL])
    ada_in = [din("ada_w%d" % l, [128, KC, 6 * D]) for l in range(2)]
    wup_in = [din("w_up%d" % l, [NF, 128, KC, 256]) for l in range(2)]
    wdn_in = [din("w_dn%d" % l, [KC, 128, NF, 128]) for l in range(2)]
    win0_in = din("att_w_in", [128, KC, 1440])
    wqb_in = din("mla_w_qb", [128, 3, 768])
    wkvb_in = din("mla_w_kvb", [128, 2, 1024])
    wout0_in = din("att_w_out", [128, KC, D])
    win1_in = din("ml_w_in", [128, KC, 3104])
    wout1_in = din("ml_w_out", [128, KC, D])
    yT_out = nc.dram_tensor("yT", [128, KC, TL], F32, kind="ExternalOutput").ap()

    S = Sched(nc)
    with contextlib.ExitStack() as st:
        arena = st.enter_context(nc.sbuf_tensor("arena", [128, ARENA_BYTES // 2], BF16))
        psb = [st.enter_context(nc.psum_tensor("ps%d" % i, [128, 512], F32)) for i in range(8)]

        def view(off_bytes, nelem, dt):
            assert off_bytes % 4 == 0
            if dt == BF16:
                return arena[:, off_bytes // 2: off_bytes // 2 + nelem]
            return arena[:, off_bytes // 2: off_bytes // 2 + 2 * nelem].bitcast(F32)

        class Bump:
            def __init__(self, start, limit=ARENA_BYTES):
                self.p = start
                self.limit = limit

            def take(self, nelem, dt):
                nb = nelem * (2 if dt == BF16 else 4)
                nb = (nb + 31) // 32 * 32
                v = view(self.p, nelem, dt)
                self.p += nb
                assert self.p <= self.limit, ("arena overflow", self.p, self.limit)
                return v

        pers = Bump(0)
        xT = pers.take(KC * T, F32).rearrange("p (k t) -> p k t", k=KC)
        H_OFF = pers.p
        hT = pers.take(KC * T, BF16).rearrange("p (k t) -> p k t", k=KC)
        H_END = pers.p
        vecs = pers.take(nv, F32)
        consts_b = pers.take(8 * 128, BF16).rearrange("p (a b) -> p a b", a=8)
        ones_b = pers.take(128, BF16)
        epsT = pers.take(1, F32)
        modT = pers.take(2 * 48 * 2, F32).rearrange("p (l j c) -> p l j c", l=2, j=48)
        modA = pers.take(2 * 2 * 8 * 2, F32).rearrange("p (l n k c) -> p l n k c", l=2, n=2, k=8)
        c2 = pers.take(KC * 2, F32).rearrange("p (k c) -> p k c", k=KC)
        sT = pers.take(KC * 2, F32).rearrange("p (k c) -> p k c", k=KC)
        PH = pers.p

        def V(name, n=1, j=0):
            return vecs[:, voff[name] + j: voff[name] + j + n]

        TGS = [(0, TC, 1)] + [(TC + 512 * i, 512, 0) for i in range(4)]
        LAT = TGS[1:]

        def gidx(tok):
            return 0 if tok < TC else 1 + (tok - TC) // 512

        def MM(ps, pairs, r, w):
            def f(e):
                ins = None
                for i, (a, b) in enumerate(pairs):
                    ins = e.matmul(ps, lhsT=a, rhs=b, start=(i == 0), stop=(i == len(pairs) - 1))
                return ins
            S.pe(f, r, w)

        def LW(dst, src, key):
            S.dma('pool', lambda e: e.dma_start(out=dst, in_=src), w=[key])

        B0 = Bump(PH)
        consts_f = B0.take(8 * 128, F32).rearrange("p (a b) -> p a b", a=8)
        S.dma('sp', lambda e: e.dma_start(out=vecs, in_=vecs_in), w=['vecs'])
        S.dma('sp', lambda e: e.dma_start(out=consts_f, in_=consts_in), w=['consts_f'])
        S.dma('sp', lambda e: e.dma_start(out=c2, in_=c_in), w=['c2'])
        for k in range(KC):
            S.dma('sp', lambda e, k=k: e.dma_start(out=xT[:, k, :], in_=xT_in[:, k, :]),
                  w=[('x', k, g) for g in range(5)])
        S.dve(lambda e: e.tensor_copy(out=consts_b, in_=consts_f), r=['consts_f'], w=['consts_b'])
        S.pool(lambda e: e.memset(ones_b, 1.0), w=['ones_b'])
        S.pool(lambda e: e.memset(epsT, EPS), w=['epsT'])
        S.act(lambda e: e.activation(out=sT, in_=c2, func=AF.Exp, scale=-1.0), r=['c2'], w=['sT'])
        S.dve(lambda e: e.tensor_scalar(out=sT, in0=sT, scalar1=1.0, scalar2=None, op0=ALU.add), r=['sT'], w=['sT'])
        S.dve(lambda e: e.reciprocal(out=sT, in_=sT), r=['sT'], w=['sT'])
        S.dve(lambda e: e.tensor_tensor(out=sT, in0=sT, in1=c2, op=ALU.mult), r=['sT', 'c2'], w=['sT'])

        def mod_gen(l, stage, pw):
            nb = pw // 128
            for pi in range(6 * D // pw):
                b = pi % 2
                stg = stage[b]
                S.dma('sp', lambda e, stg=stg, pi=pi: e.dma_start(out=stg, in_=ada_in[l][:, :, pi * pw:(pi + 1) * pw]),
                      w=[('ada', b)])
                pso = psb[7][:, 0:2 * nb]

                def mm(e, stg=stg, pso=pso):
                    ins = None
                    for m in range(nb):
                        for k in range(KC):
                            ins = e.matmul(pso[:, 2 * m:2 * m + 2], lhsT=stg[:, k, m * 128:(m + 1) * 128],
                                           rhs=sT[:, k, :], start=(k == 0), stop=(k == KC - 1))
                    return ins
                S.pe(mm, r=[('ada', b), 'sT'], w=[('ps', 7)])
                S.dve(lambda e, pi=pi, pso=pso: e.tensor_tensor(
                    out=modT[:, l, nb * pi:nb * pi + nb, :], in0=pso.rearrange("p (a b) -> p a b", a=nb),
                    in1=V('adab%d' % l, nb, nb * pi).unsqueeze(2).broadcast_to([128, nb, 2]), op=ALU.add),
                    r=[('ps', 7), 'vecs'], w=[('mod', l)])
                yield
            for n_idx in range(2):
                v = 1 if n_idx == 0 else 4
                gname = ('n1g%d' if n_idx == 0 else 'n2g%d') % l
                S.dve(lambda e, v=v, gname=gname, n_idx=n_idx: e.scalar_tensor_tensor(
                    out=modA[:, l, n_idx, :, :], in0=modT[:, l, 8 * v:8 * v + 8, :], scalar=1.0,
                    in1=V(gname, 8).unsqueeze(2).broadcast_to([128, 8, 2]), op0=ALU.add, op1=ALU.mult),
                    r=[('mod', l), 'vecs'], w=[('mod', l)])

        def emit_mod(l, B):
            stage = [B.take(KC * 512, F32).rearrange("p (k n) -> p k n", k=KC) for _ in range(2)]
            for _ in mod_gen(l, stage, 512):
                pass

        def modv(l, v, k, c):
            return modT[:, l, 8 * v + k, c:c + 1]

        def emit_norm(l, n_idx, groups, B):
            v_shift = 0 if n_idx == 0 else 3
            sq = [B.take(512, BF16) for _ in range(2)]
            lnb = B.take(512, F32)
            rstd = B.take(512, F32)
            tmp = [B.take(512, F32) for _ in range(2)]
            for (s, n, mc) in groups:
                g = gidx(s)
                pss = psb[6][:, 0:n]
                for k in range(KC):
                    S.act(lambda e, k=k, s=s, n=n: e.activation(out=sq[k % 2][:, 0:n], in_=xT[:, k, s:s + n], func=AF.Square),
                          r=[('x', k, g)], w=[('nsq', k % 2)])
                    S.pe(lambda e, k=k, n=n, pss=pss: e.matmul(pss, lhsT=ones_b, rhs=sq[k % 2][:, 0:n],
                                                              start=(k == 0), stop=(k == KC - 1)),
                         r=[('nsq', k % 2), 'ones_b'], w=[('ps', 6)])
                S.act(lambda e, n=n, pss=pss: e.activation(out=lnb[:, 0:n], in_=pss, func=AF.Ln, scale=1.0 / D, bias=epsT),
                      r=[('ps', 6), 'epsT'], w=['nln'])
                S.act(lambda e, n=n: e.activation(out=rstd[:, 0:n], in_=lnb[:, 0:n], func=AF.Exp, scale=-0.5),
                      r=['nln'], w=['nrstd'])
                for k in range(KC):
                    S.dve(lambda e, k=k, s=s, n=n, mc=mc: e.scalar_tensor_tensor(
                        out=tmp[k % 2][:, 0:n], in0=xT[:, k, s:s + n], scalar=modA[:, l, n_idx, k, mc:mc + 1],
                        in1=rstd[:, 0:n], op0=ALU.mult, op1=ALU.mult),
                        r=[('x', k, g), ('mod', l), 'nrstd'], w=[('ntmp', k % 2)])
                    S.pool(lambda e, k=k, s=s, n=n, mc=mc: e.tensor_scalar(
                        out=hT[:, k, s:s + n], in0=tmp[k % 2][:, 0:n], scalar1=modv(l, v_shift, k, mc), scalar2=None,
                        op0=ALU.add),
                        r=[('ntmp', k % 2), ('mod', l)], w=[('h', k, g)])

        def emit_ffn(l, groups, B, side=None):
            wp = [B.take(KC * 256, BF16).rearrange("p (k n) -> p k n", k=KC) for _ in range(3)]
            wd = [B.take(NF * 128, BF16).rearrange("p (f n) -> p f n", f=NF) for _ in range(2)]
            cbuf = [B.take(512, F32) for _ in range(2)]
            ebuf = [B.take(512, F32) for _ in range(2)]
            u = B.take(NF * 512, BF16).rearrange("p (f n) -> p f n", f=NF)
            wctr = 0
            dctr = 0
            for (s, n, mc) in groups:
                g = gidx(s)
                has_l = (s != 0 and s != TC)
                has_r = (s + n != TC and s + n != T)
                hkeys = [('h', k, g) for k in range(KC)]
                hkeys_halo = list(hkeys)
                if has_l:
                    hkeys_halo += [('h', k, gidx(s - 1)) for k in range(KC)]
                if has_r:
                    hkeys_halo += [('h', k, gidx(s + n)) for k in range(KC)]
                for f in range(NF):
                    if side is not None:
                        next(side, None)
                    wb = wctr % 3
                    wctr += 1
                    par = f % 2
                    LW(wp[wb], wup_in[l][f], ('wup', wb))
                    psg = psb[par][:, 0:n]
                    psv = psb[2 + par][:, 0:n]
                    MM(psg, [(wp[wb][:, k, 0:128], hT[:, k, s:s + n]) for k in range(KC)], [('wup', wb)] + hkeys, [('ps', par)])
                    MM(psv, [(wp[wb][:, k, 128:256], hT[:, k, s:s + n]) for k in range(KC)], [('wup', wb)] + hkeys, [('ps', 2 + par)])
                    psh = psb[4][:, 2 * par:2 * par + 2]
                    if has_l and has_r:
                        hsl = lambda k: hT[:, k, s - 1:s + n + 1:n + 1]
                        hc = 2
                    elif has_l:
                        hsl = lambda k: hT[:, k, s - 1:s]
                        hc = 1
                    elif has_r:
                        hsl = lambda k: hT[:, k, s + n:s + n + 1]
                        hc = 1
                    else:
                        hc = 0
                    if hc:
                        MM(psh[:, 0:hc], [(wp[wb][:, k, 0:128], hsl(k)) for k in range(KC)], [('wup', wb)] + hkeys_halo, [('ps4', par)])
                    cb = cbuf[par]
                    eb = ebuf[par]
                    S.act(lambda e, cb=cb, psg=psg, f=f, n=n: e.activation(
                        out=cb[:, 0:n], in_=psg, func=AF.Identity, scale=V('fcw%d_1' % l, 1, f), bias=V('fcb%d' % l, 1, f)),
                        r=[('ps', par), 'vecs'], w=[('cb', par)])
                    S.dve(lambda e, cb=cb, psg=psg, f=f, n=n: e.scalar_tensor_tensor(
                        out=cb[:, 1:n], in0=psg[:, 0:n - 1], scalar=V('fcw%d_0' % l, 1, f), in1=cb[:, 1:n],
                        op0=ALU.mult, op1=ALU.add), r=[('ps', par), 'vecs', ('cb', par)], w=[('cb', par)])
                    S.dve(lambda e, cb=cb, psg=psg, f=f, n=n: e.scalar_tensor_tensor(
                        out=cb[:, 0:n - 1], in0=psg[:, 1:n], scalar=V('fcw%d_2' % l, 1, f), in1=cb[:, 0:n - 1],
                        op0=ALU.mult, op1=ALU.add), r=[('ps', par), 'vecs', ('cb', par)], w=[('cb', par)])
                    if has_l:
                        S.dve(lambda e, cb=cb, psh=psh, f=f: e.scalar_tensor_tensor(
                            out=cb[:, 0:1], in0=psh[:, 0:1], scalar=V('fcw%d_0' % l, 1, f), in1=cb[:, 0:1],
                            op0=ALU.mult, op1=ALU.add), r=[('ps4', par), 'vecs', ('cb', par)], w=[('cb', par)])
                    if has_r:
                        c_r = 1 if has_l else 0
                        S.dve(lambda e, cb=cb, psh=psh, f=f, n=n, c_r=c_r: e.scalar_tensor_tensor(
                            out=cb[:, n - 1:n], in0=psh[:, c_r:c_r + 1], scalar=V('fcw%d_2' % l, 1, f), in1=cb[:, n - 1:n],
                            op0=ALU.mult, op1=ALU.add), r=[('ps4', par), 'vecs', ('cb', par)], w=[('cb', par)])
                    S.act(lambda e, cb=cb, eb=eb, n=n: e.activation(out=eb[:, 0:n], in_=cb[:, 0:n], func=AF.Exp, scale=-1.0),
                          r=[('cb', par)], w=[('eb', par)])
                    S.pool(lambda e, eb=eb, n=n: e.tensor_scalar(out=eb[:, 0:n], in0=eb[:, 0:n], scalar1=1.0, scalar2=None, op0=ALU.add),
                           r=[('eb', par)], w=[('eb', par)])
                    S.dve(lambda e, eb=eb, n=n: e.reciprocal(out=eb[:, 0:n], in_=eb[:, 0:n]), r=[('eb', par)], w=[('eb', par)])
                    S.dve(lambda e, cb=cb, psv=psv, n=n: e.tensor_tensor(out=cb[:, 0:n], in0=psv, in1=cb[:, 0:n], op=ALU.mult),
                          r=[('ps', 2 + par), ('cb', par)], w=[('cb', par)])
                    S.pool(lambda e, cb=cb, eb=eb, f=f, n=n: e.tensor_tensor(out=u[:, f, 0:n], in0=cb[:, 0:n], in1=eb[:, 0:n], op=ALU.mult),
                           r=[('cb', par), ('eb', par)], w=[('u', f)])
                for d in range(KC):
                    db = dctr % 2
                    dctr += 1
                    LW(wd[db], wdn_in[l][d], ('wdn', db))
                    psd = psb[5 + db][:, 0:n]
                    MM(psd, [(wd[db][:, f, :], u[:, f, 0:n]) for f in range(NF)], [('wdn', db)] + [('u', f) for f in range(NF)], [('ps', 5 + db)])
                    S.dve(lambda e, psd=psd, d=d, s=s, n=n, mc=mc: e.scalar_tensor_tensor(
                        out=xT[:, d, s:s + n], in0=psd, scalar=modv(l, 5, d, mc), in1=xT[:, d, s:s + n],
                        op0=ALU.mult, op1=ALU.add), r=[('ps', 5 + db), ('mod', l), ('x', d, g)], w=[('x', d, g)])

        def emit_attn(l):
            A = Bump(PH)
            cosT = A.take(TL, F32)
            sinT = A.take(TL, F32)
            S.dma('sp', lambda e: e.dma_start(out=cosT, in_=cos_in), w=['cosT'])
            S.dma('sp', lambda e: e.dma_start(out=sinT, in_=sin_in), w=['sinT'])
            HF = [dict(Xg=A.take(512, F32), sqb=A.take(512, BF16), lnb=A.take(512, F32), rstd=A.take(512, F32),
                       xgb=A.take(512, BF16), tmp2=A.take(512, F32)) for _ in range(2)]
            lnb = HF[0]['lnb']
            rstd = HF[0]['rstd']
            rs = A.take(512, F32)
            rinv = A.take(512, F32)
            tmpo = A.take(512, BF16)
            A_COMMON = A.p
            hf_ctr = [0]

            def head_finish(psx, pskey, dk, n, gname, rope, dst, dstkeys, c0):
                i = hf_ctr[0] % 2
                hf_ctr[0] += 1
                sc = HF[i]
                Xg, sqb, lnb, rstd, xgb, tmp2 = sc['Xg'], sc['sqb'], sc['lnb'], sc['rstd'], sc['xgb'], sc['tmp2']
                kx, ksq, kln, krs, kxb, kt2 = [(nm, i) for nm in ('hf_xg', 'hf_sq', 'hf_ln', 'hf_rstd', 'hf_xgb', 'hf_t2')]
                S.act(lambda e: e.activation(out=Xg[0:dk, 0:n], in_=psx, func=AF.Identity, scale=V(gname)[0:dk, :]),
                      r=[pskey, 'vecs'], w=[kx])
                S.act(lambda e: e.activation(out=sqb[0:dk, 0:n], in_=psx, func=AF.Square), r=[pskey], w=[ksq])
                pss = psb[1][0:dk, 0:n]
                MM(pss, [(ones_b[0:dk, 0:dk], sqb[0:dk, 0:n])], [ksq, 'ones_b'], [('ps', 1)])
                S.act(lambda e: e.activation(out=lnb[0:dk, 0:n], in_=pss, func=AF.Ln, scale=1.0 / dk, bias=epsT[0:dk, :]),
                      r=[('ps', 1), 'epsT'], w=[kln])
                S.act(lambda e: e.activation(out=rstd[0:dk, 0:n], in_=lnb[0:dk, 0:n], func=AF.Exp, scale=-0.5),
                      r=[kln], w=[krs])
                if rope is not None:
                    r0, r1, cidx = rope
                    S.dve(lambda e: e.tensor_copy(out=xgb[0:dk, 0:n], in_=Xg[0:dk, 0:n]), r=[kx], w=[kxb])
                    psr = psb[2][0:dk, 0:n]
                    MM(psr, [(consts_b[0:dk, cidx, 0:dk], xgb[0:dk, 0:n])], [kxb, 'consts_b'], [('ps', 2)])
                    S.dve(lambda e: e.tensor_tensor(out=Xg[r0:r1, 0:n], in0=Xg[r0:r1, 0:n], in1=cosT[r0:r1, c0:c0 + n], op=ALU.mult),
                          r=[kx, kxb, 'cosT'], w=[kx])
                    S.dve(lambda e: e.tensor_tensor(out=tmp2[r0:r1, 0:n], in0=psr[r0:r1, :], in1=sinT[r0:r1, c0:c0 + n], op=ALU.mult),
                          r=[('ps', 2), 'sinT'], w=[kt2])
                    S.pool(lambda e: e.tensor_tensor(out=Xg[r0:r1, 0:n], in0=Xg[r0:r1, 0:n], in1=tmp2[r0:r1, 0:n], op=ALU.add),
                           r=[kx, kt2], w=[kx])
                S.dve(lambda e: e.tensor_tensor(out=dst, in0=Xg[0:dk, 0:n], in1=rstd[0:dk, 0:n], op=ALU.mult),
                      r=[kx, krs], w=dstkeys)

            def attention(qT, kT, Vp, dk, scale, O2, odd, pT):
                for (s, n, mc) in TGS:
                    g = gidx(s)
                    kts = list(range(18)) if mc == 0 else [0, 1]
                    pso = psb[5 + (g % 2)]
                    psokey = ('ps', 5 + (g % 2))

                    def smm(kt, s=s, n=n, g=g):
                        MM(psb[3 + (kt % 2)][:, 0:n], [(kT[0:dk, kt * 128:(kt + 1) * 128], qT[0:dk, s:s + n])],
                           [('kT', gidx(kt * 128)), ('qT', g)], [('ps', 3 + (kt % 2))])
                    smm(kts[0])
                    for i, kt in enumerate(kts):
                        if i + 1 < len(kts):
                            smm(kts[i + 1])
                        pt = pT[kt % 2]
                        S.act(lambda e, kt=kt, pt=pt, n=n: e.activation(out=pt[:, 0:n], in_=psb[3 + (kt % 2)][:, 0:n], func=AF.Exp, scale=scale),
                              r=[('ps', 3 + (kt % 2))], w=[('pT', kt % 2)])
                        S.pe(lambda e, kt=kt, pt=pt, n=n, i=i, pso=pso, nk=len(kts): e.matmul(
                            pso[:, 0:n], lhsT=Vp[:, kt, :], rhs=pt[:, 0:n], start=(i == 0), stop=(i == nk - 1)),
                            r=[('pT', kt % 2), ('Vp', kt)], w=[psokey])
                    S.act(lambda e, pso=pso, n=n: e.copy(out=rs[0:64, 0:n], in_=pso[64:128, 0:n]), r=[psokey], w=['at_rs'])
                    S.dve(lambda e, n=n: e.reciprocal(out=rinv[0:64, 0:n], in_=rs[0:64, 0:n]), r=['at_rs'], w=['at_rinv'])
                    if not odd:
                        S.dve(lambda e, pso=pso, s=s, n=n: e.tensor_tensor(out=O2[0:64, s:s + n], in0=pso[0:64, 0:n], in1=rinv[0:64, 0:n], op=ALU.mult),
                              r=[psokey, 'at_rinv'], w=[('O2', g)])
                    else:
                        S.dve(lambda e, pso=pso, n=n: e.tensor_tensor(out=tmpo[0:64, 0:n], in0=pso[0:64, 0:n], in1=rinv[0:64, 0:n], op=ALU.mult),
                              r=[psokey, 'at_rinv'], w=['at_tmpo'])
                        S.act(lambda e, s=s, n=n: e.copy(out=O2[64:128, s:s + n], in_=tmpo[0:64, 0:n]), r=['at_tmpo'], w=[('O2', g)])

            def out_proj(pair, O2, wo):
                wb = wo[pair % 2]
                LW(wb, wout0_in[:, pair, :], ('wo', pair % 2))
                for (s, n, mc) in TGS:
                    g = gidx(s)
                    for d in range(KC):
                        bank = 7 if d % 2 == 0 else 0
                        psd = psb[bank][:, 0:n]
                        MM(psd, [(wb[:, d * 128:(d + 1) * 128], O2[:, s:s + n])], [('wo', pair % 2), ('O2', g)], [('ps', bank)])
                        S.dve(lambda e, psd=psd, d=d, s=s, n=n, mc=mc: e.scalar_tensor_tensor(
                            out=xT[:, d, s:s + n], in0=psd, scalar=modv(l, 2, d, mc), in1=xT[:, d, s:s + n],
                            op0=ALU.mult, op1=ALU.add), r=[('ps', bank), ('mod', l), ('x', d, g)], w=[('x', d, g)])

            Bn = Bump(A_COMMON)
            emit_norm(l, 0, TGS, Bn)
            S.fence()

            G = Bump(A_COMMON)
            wg = G.take(KC * 768, BF16).rearrange("p (k n) -> p k n", k=KC)
            qT = G.take(T, BF16)
            kT = G.take(T, BF16)
            Vp = G.take(18 * 128, BF16).rearrange("p (t n) -> p t n", t=18)
            O2 = G.take(T, BF16)
            pT = [G.take(512, BF16) for _ in range(2)]
            wo = [G.take(D, BF16) for _ in range(2)]
            LW(wg, win0_in[:, :, W_GQ:1440], 'wg')
            S.pool(lambda e: e.memset(Vp[:, :, 64:128], 1.0), w=[('Vp', t) for t in range(18)])
            allh = lambda g: [('h', k, g) for k in range(KC)]
            for kv in ([] if flags.get('skip_gqa') else range(2)):
                for (s, n, mc) in TGS:
                    g = gidx(s)
                    c_off = (W_GK - W_GQ) + kv * 64
                    MM(psb[0][0:64, 0:n], [(wg[:, k, c_off:c_off + 64], hT[:, k, s:s + n]) for k in range(KC)], ['wg'] + allh(g), [('ps', 0)])
                    head_finish(psb[0][0:64, 0:n], ('ps', 0), 64, n, 'gkg', (0, 64, C_RG) if mc == 0 else None,
                                kT[0:64, s:s + n], [('kT', g)], s - TC)
                for t in range(18):
                    c_off = (W_GV - W_GQ) + kv * 64
                    g = gidx(t * 128)
                    MM(psb[1][:, 0:64], [(hT[:, k, t * 128:(t + 1) * 128], wg[:, k, c_off:c_off + 64]) for k in range(KC)], ['wg'] + allh(g), [('ps', 1)])
                    S.act(lambda e, t=t: e.copy(out=Vp[:, t, 0:64], in_=psb[1][:, 0:64]), r=[('ps', 1)], w=[('Vp', t)])
                for hq in range(4):
                    h = kv * 4 + hq
                    for (s, n, mc) in TGS:
                        g = gidx(s)
                        c_off = h * 64
                        MM(psb[0][0:64, 0:n], [(wg[:, k, c_off:c_off + 64], hT[:, k, s:s + n]) for k in range(KC)], ['wg'] + allh(g), [('ps', 0)])
                        head_finish(psb[0][0:64, 0:n], ('ps', 0), 64, n, 'gqg', (0, 64, C_RG) if mc == 0 else None,
                                    qT[0:64, s:s + n], [('qT', g)], s - TC)
                    attention(qT, kT, Vp, 64, 64 ** -0.5, O2, h % 2 == 1, pT)
                    if flags.get('dbg_attn') and h == flags.get('dbg_head', 0):
                        allx = [('x', k, g) for k in range(KC) for g in range(5)]
                        rk = allx + [('h', k, g) for k in range(KC) for g in range(5)] + [('kT', g) for g in range(5)] + [('qT', g) for g in range(5)] + [('Vp', t) for t in range(18)] + [('O2', g) for g in range(5)]
                        S.fence()
                        S.dve(lambda e: e.tensor_copy(out=xT[:, 0, :], in_=hT[:, 0, :]), r=rk, w=allx)
                        S.dve(lambda e: e.tensor_copy(out=xT[0:64, 1, :], in_=kT[0:64, :]), r=rk, w=allx)
                        S.dve(lambda e: e.tensor_copy(out=xT[64:128, 1, :], in_=qT[0:64, :]), r=rk, w=allx)
                        S.dve(lambda e: e.tensor_copy(out=xT[:, 2, 0:1152].rearrange("p (t n) -> p t n", t=18), in_=Vp[:, :, 0:64]), r=rk, w=allx)
                        S.dve(lambda e: e.tensor_copy(out=xT[:, 3, :], in_=O2[:, :]), r=rk, w=allx)
                        S.fence()
                        raise StopEmit()
                    if h % 2 == 1:
                        out_proj(4 + h // 2, O2, wo)
            S.fence()
            if flags.get('stop_gqa_done'):
                raise StopEmit()

            Lb = Bump(A_COMMON)
            cqn = Lb.take(3 * T, BF16).rearrange("p (j t) -> p j t", j=3)
            ckvn = Lb.take(2 * T, BF16).rearrange("p (j t) -> p j t", j=2)
            krT = Lb.take(T, BF16)
            L_END = Lb.p
            wl = Lb.take(KC * 672, BF16).rearrange("p (k n) -> p k n", k=KC)
            sq3 = Lb.take(512, BF16)
            LW(wl, win0_in[:, :, 0:672], 'wl')
            for (s, n, mc) in TGS:
                g = gidx(s)
                for (name, j0, nj, off, gname, dstbuf) in (('cq', 0, 3, W_CQ, 'qag', cqn), ('ckv', 3, 2, W_CKV, 'kvag', ckvn)):
                    dim = nj * 128
                    for j in range(nj):
                        MM(psb[j][:, 0:n], [(wl[:, k, off + j * 128: off + (j + 1) * 128], hT[:, k, s:s + n]) for k in range(KC)],
                           ['wl'] + allh(g), [('ps', j)])
                    pss = psb[3][:, 0:n]
                    for j in range(nj):
                        S.act(lambda e, j=j, n=n: e.activation(out=sq3[:, 0:n], in_=psb[j][:, 0:n], func=AF.Square), r=[('ps', j)], w=['sq3'])
                        S.pe(lambda e, j=j, n=n, nj=nj, pss=pss: e.matmul(pss, lhsT=ones_b, rhs=sq3[:, 0:n], start=(j == 0), stop=(j == nj - 1)),
                             r=['sq3', 'ones_b'], w=[('ps', 3)])
                    S.act(lambda e, n=n, pss=pss, dim=dim: e.activation(out=lnb[:, 0:n], in_=pss, func=AF.Ln, scale=1.0 / dim, bias=epsT),
                          r=[('ps', 3), 'epsT'], w=[('hf_ln', 0)])
                    S.act(lambda e, n=n: e.activation(out=rstd[:, 0:n], in_=lnb[:, 0:n], func=AF.Exp, scale=-0.5), r=[('hf_ln', 0)], w=[('hf_rstd', 0)])
                    for j in range(nj):
                        S.dve(lambda e, j=j, s=s, n=n, gname=gname, dstbuf=dstbuf: e.scalar_tensor_tensor(
                            out=dstbuf[:, j, s:s + n], in0=psb[j][:, 0:n], scalar=V(gname, 1, j), in1=rstd[:, 0:n],
                            op0=ALU.mult, op1=ALU.mult), r=[('ps', j), 'vecs', ('hf_rstd', 0)], w=[(name, g)])
                MM(psb[4][0:32, 0:n], [(wl[:, k, W_KR:W_KR + 32], hT[:, k, s:s + n]) for k in range(KC)], ['wl'] + allh(g), [('ps', 4)])
                S.act(lambda e, s=s, n=n: e.copy(out=krT[0:32, s:s + n], in_=psb[4][0:32, 0:n]), r=[('ps', 4)], w=[('kr', g)])
            S.fence()

            M = Bump(H_OFF, H_END)
            wqb = M.take(3 * 768, BF16).rearrange("p (j n) -> p j n", j=3)
            wkn = M.take(2 * 8 * 96, BF16).rearrange("p (j h n) -> p j h n", j=2, h=8)
            wv = M.take(2 * 512, BF16).rearrange("p (j n) -> p j n", j=2)
            qTm = M.take(T, BF16)
            kTm = M.take(T, BF16)
            Vpm = M.take(18 * 128, BF16).rearrange("p (t n) -> p t n", t=18)
            O2m = M.take(T, BF16)
            pTm = [M.take(512, BF16) for _ in range(2)]
            wom = [M.take(D, BF16) for _ in range(2)]
            S.pool(lambda e: e.memset(wkn, 0.0), w=['wkn'])
            S.pool(lambda e: e.memset(Vpm[:, :, 64:128], 1.0), w=[('Vp', t) for t in range(18)])
            LW(wqb, wqb_in, 'wqb')
            kvv = wkvb_in.rearrange("p j (h c) -> p j h c", c=128)
            for j in range(2):
                LW(wkn[:, j, :, 0:64], kvv[:, j, :, 0:64], 'wkn')
                LW(wv[:, j, :].rearrange("p (h c) -> p h c", c=64), kvv[:, j, :, 64:128], 'wv')
            for h in range(8):
                for (s, n, mc) in TGS:
                    g = gidx(s)
                    MM(psb[0][0:96, 0:n], [(wkn[:, j, h, :], ckvn[:, j, s:s + n]) for j in range(2)] + [(consts_b[0:32, C_PSEL, 0:96], krT[0:32, s:s + n])],
                       ['wkn', ('ckv', g), ('kr', g), 'consts_b'], [('ps', 0)])
                    head_finish(psb[0][0:96, 0:n], ('ps', 0), 96, n, 'mkg', (64, 96, C_RM) if mc == 0 else None,
                                kTm[0:96, s:s + n], [('kT', g)], s - TC)
                for t in range(18):
                    g = gidx(t * 128)
                    MM(psb[1][:, 0:64], [(ckvn[:, j, t * 128:(t + 1) * 128], wv[:, j, h * 64:(h + 1) * 64]) for j in range(2)], ['wv', ('ckv', g)], [('ps', 1)])
                    S.act(lambda e, t=t: e.copy(out=Vpm[:, t, 0:64], in_=psb[1][:, 0:64]), r=[('ps', 1)], w=[('Vp', t)])
                for (s, n, mc) in TGS:
                    g = gidx(s)
                    MM(psb[0][0:96, 0:n], [(wqb[:, j, h * 96:(h + 1) * 96], cqn[:, j, s:s + n]) for j in range(3)], ['wqb', ('cq', g)], [('ps', 0)])
                    head_finish(psb[0][0:96, 0:n], ('ps', 0), 96, n, 'mqg', (64, 96, C_RM) if mc == 0 else None,
                                qTm[0:96, s:s + n], [('qT', g)], s - TC)
                attention(qTm, kTm, Vpm, 96, 96 ** -0.5, O2m, h % 2 == 1, pTm)
                if flags.get('dbg_mla') is not None and h == flags.get('dbg_mla'):
                    allx = [('x', k, g) for k in range(KC) for g in range(5)]
                    rk = list(allx)
                    S.fence()
                    S.dve(lambda e: e.tensor_copy(out=xT[0:96, 1, :], in_=kTm[0:96, :]), r=rk, w=allx)
                    S.dve(lambda e: e.tensor_copy(out=xT[0:96, 2, :], in_=qTm[0:96, :]), r=rk, w=allx)
                    S.dve(lambda e: e.tensor_copy(out=xT[:, 3, 0:1152].rearrange("p (t n) -> p t n", t=18), in_=Vpm[:, :, 0:64]), r=rk, w=allx)
                    S.dve(lambda e: e.tensor_copy(out=xT[:, 4, :], in_=O2m[:, :]), r=rk, w=allx)
                    S.dve(lambda e: e.tensor_copy(out=xT[:, 5, :], in_=cqn[:, 0, :]), r=rk, w=allx)
                    S.dve(lambda e: e.tensor_copy(out=xT[:, 6, :], in_=ckvn[:, 0, :]), r=rk, w=allx)
                    S.dve(lambda e: e.tensor_copy(out=xT[0:32, 7, :], in_=krT[0:32, :]), r=rk, w=allx)
                    S.fence()
                    raise StopEmit()
                if h % 2 == 1:
                    out_proj(h // 2, O2m, wom)
            S.fence()

        def emit_mlstm(l):
            A = Bump(PH)
            emit_norm(l, 0, TGS, Bump(ARENA_BYTES - 16384))
            S.fence()
            cf = A.take(4 * 128, F32).rearrange("p (a b) -> p a b", a=4)
            ones_f = A.take(64, F32)
            GI = A.take(18 * 32, F32).rearrange("p (t c) -> p t c", t=18)
            SP = A.take(18 * 32, F32).rearrange("p (t c) -> p t c", t=18)
            BETA = A.take(18 * 16, F32).rearrange("p (t c) -> p t c", t=18)
            EPSI = A.take(18 * 16, F32).rearrange("p (t c) -> p t c", t=18)
            WW = A.take(18 * 16, F32).rearrange("p (t c) -> p t c", t=18)
            DEC = A.take(18 * 16, F32).rearrange("p (t c) -> p t c", t=18)
            t16 = [A.take(16, F32) for _ in range(2)]
            wgate = A.take(KC * 32, BF16).rearrange("p (k n) -> p k n", k=KC)
            S.dma('sp', lambda e: e.dma_start(out=cf, in_=consts_in[:, 1:5, :]), w=['cf'])
            S.pool(lambda e: e.memset(ones_f, 1.0), w=['ones_f'])
            LW(wgate, win1_in[:, :, 3072:3104], 'wgate')
            allh = lambda g: [('h', k, g) for k in range(KC)]
            T3 = A.take(18 * 16, F32).rearrange("p (t c) -> p t c", t=18)
            T4 = A.take(18 * 16, F32).rearrange("p (t c) -> p t c", t=18)
            for half in range(2):
                for tt in range(9):
                    t = half * 9 + tt
                    g = gidx(t * 128)
                    MM(psb[1 + half][:, 32 * tt:32 * tt + 32], [(hT[:, k, t * 128:(t + 1) * 128], wgate[:, k, :]) for k in range(KC)],
                       ['wgate'] + allh(g), [('ps', 1 + half)])
                S.dve(lambda e, half=half: e.tensor_tensor(
                    out=GI[:, 9 * half:9 * half + 9, :], in0=psb[1 + half][:, 0:288].rearrange("p (t c) -> p t c", t=9),
                    in1=V('gateb', 32).unsqueeze(1).broadcast_to([128, 9, 32]), op=ALU.add),
                    r=[('ps', 1 + half), 'vecs'], w=['GI'])
            S.act(lambda e: e.activation(out=SP, in_=GI, func=AF.Exp, scale=-1.0), r=['GI'], w=['SP'])
            S.act(lambda e: e.activation(out=SP, in_=SP, func=AF.Ln, bias=1.0), r=['SP'], w=['SP'])
            for t in range(18):
                for (bank, coff, cidx, fcol) in ((3, 0, C_LE, 8), (3, 144, C_GT, 8), (4, 0, C_GE, 24), (4, 144, C_LT, 24)):
                    MM(psb[bank][:, coff + 8 * t:coff + 8 * t + 8], [(cf[:, cidx - 1, :], SP[:, t, fcol:fcol + 8])], ['cf', 'SP'], [('ps', bank)])
                for dr, fcol in enumerate((8, 24)):
                    MM(psb[5][0:64, 16 * t + 8 * dr:16 * t + 8 * dr + 8], [(ones_f, SP[:, t, fcol:fcol + 8])], ['ones_f', 'SP'], [('ps', 5)])
            for dr, (bank, icol) in enumerate(((3, 0), (4, 16))):
                pb = psb[bank][:, 0:144].rearrange("p (t c) -> p t c", t=18)
                prb = psb[bank][:, 144:288].rearrange("p (t c) -> p t c", t=18)
                cs = slice(8 * dr, 8 * dr + 8)
                S.dve(lambda e, pb=pb, icol=icol, cs=cs: e.tensor_tensor(out=T3[:, :, cs], in0=pb, in1=GI[:, :, icol:icol + 8], op=ALU.add),
                      r=[('ps', bank), 'GI'], w=[('T3', dr)])
                S.act(lambda e, cs=cs: e.activation(out=BETA[:, :, cs], in_=T3[:, :, cs], func=AF.Exp), r=[('T3', dr)], w=['BETA'])
                S.act(lambda e, pb=pb, cs=cs: e.activation(out=EPSI[:, :, cs], in_=pb, func=AF.Exp), r=[('ps', bank)], w=['EPSI'])
                S.dve(lambda e, prb=prb, icol=icol, cs=cs: e.tensor_tensor(out=T4[:, :, cs], in0=prb, in1=GI[:, :, icol:icol + 8], op=ALU.subtract),
                      r=[('ps', bank), 'GI'], w=[('T4', dr)])
                S.act(lambda e, cs=cs: e.activation(out=WW[:, :, cs], in_=T4[:, :, cs], func=AF.Exp, scale=-1.0), r=[('T4', dr)], w=['WW'])
            S.act(lambda e: e.activation(out=DEC[0:64, :, :], in_=psb[5][0:64, 0:288].rearrange("p (t c) -> p t c", t=18), func=AF.Exp, scale=-1.0),
                  r=[('ps', 5)], w=['DEC'])
            S.fence()
            raw = A.take(T, F32)
            cv = A.take(T, F32)
            qT = A.take(T, BF16)
            kT = A.take(T, BF16)
            khat_all = A.take(2 * 18 * 64, BF16)
            khat = [khat_all[:, d * 1152:(d + 1) * 1152].rearrange("p (t n) -> p t n", t=18) for d in range(2)]
            gT = khat_all[:, 0:TL]
            vp = A.take(18 * 132, BF16).rearrange("p (t n) -> p t n", t=18)
            sigT = A.take(TL, BF16)
            esc = A.take(512, F32)
            numb = [A.take(16 * 132, F32).rearrange("p (t n) -> p t n", t=16) for _ in range(2)]
            hacc = numb[0][:, :, 0:128]
            rcb = [A.take(16, F32) for _ in range(2)]
            hn = A.take(16 * 128, BF16).rearrange("p (t n) -> p t n", t=16)
            Sm = [A.take(128, BF16) for _ in range(2)]
            C32 = [A.take(132, F32) for _ in range(2)]
            Cbf = [A.take(132, BF16) for _ in range(2)]
            ss = A.take(16, F32)
            wqk = A.take(KC * 128, BF16).rearrange("p (k n) -> p k n", k=KC)
            wvh = A.take(KC * 128, BF16).rearrange("p (k n) -> p k n", k=KC)
            woh = A.take(KC * 128, BF16).rearrange("p (k n) -> p k n", k=KC)
            wout = A.take(D, BF16)
            S.pool(lambda e: e.memset(vp[:, :, 128:132], 1.0), w=[('vp', t) for t in range(18)])
            SEQ = ((0, TC), (TC, T))
            for h in range(8):
                S.fence()
                LW(wqk[:, :, 0:64], win1_in[:, :, h * 64:(h + 1) * 64], 'wqk')
                LW(wqk[:, :, 64:128], win1_in[:, :, 512 + h * 64:512 + (h + 1) * 64], 'wqk')
                LW(wvh, win1_in[:, :, 1024 + h * 128:1024 + (h + 1) * 128], 'wvh')
                LW(woh, win1_in[:, :, 2048 + h * 128:2048 + (h + 1) * 128], 'woh')
                LW(wout, wout1_in[:, h, :], 'wout')
                for (s, n, mc) in TGS:
                    g = gidx(s)
                    MM(psb[2][:, 0:n], [(wqk[:, k, :], hT[:, k, s:s + n]) for k in range(KC)], ['wqk'] + allh(g), [('ps', 2)])
                    S.act(lambda e, s=s, n=n: e.copy(out=raw[:, s:s + n], in_=psb[2][:, 0:n]), r=[('ps', 2)], w=['ml_raw'])
                S.act(lambda e, h=h: e.activation(out=cv, in_=raw, func=AF.Identity, scale=V('mqk_w1', 1, h), bias=V('mqk_b', 1, h)),
                      r=['ml_raw', 'vecs'], w=['ml_cv'])
                for (a, b) in SEQ:
                    S.dve(lambda e, a=a, b=b, h=h: e.scalar_tensor_tensor(
                        out=cv[:, a + 1:b], in0=raw[:, a:b - 1], scalar=V('mqk_w0', 1, h), in1=cv[:, a + 1:b],
                        op0=ALU.mult, op1=ALU.add), r=['ml_raw', 'ml_cv', 'vecs'], w=['ml_cv'])
                    S.dve(lambda e, a=a, b=b, h=h: e.scalar_tensor_tensor(
                        out=cv[:, a:b - 1], in0=raw[:, a + 1:b], scalar=V('mqk_w2', 1, h), in1=cv[:, a:b - 1],
                        op0=ALU.mult, op1=ALU.add), r=['ml_raw', 'ml_cv', 'vecs'], w=['ml_cv'])
                S.act(lambda e: e.activation(out=raw, in_=cv, func=AF.Exp, scale=-1.0), r=['ml_cv'], w=['ml_raw'])
                S.pool(lambda e: e.tensor_scalar(out=raw, in0=raw, scalar1=1.0, scalar2=None, op0=ALU.add), r=['ml_raw'], w=['ml_raw'])
                S.dve(lambda e: e.reciprocal(out=raw, in_=raw), r=['ml_raw'], w=['ml_raw'])
                S.dve(lambda e: e.scalar_tensor_tensor(out=qT, in0=cv, scalar=V('mqk_sc'), in1=raw, op0=ALU.mult, op1=ALU.mult),
                      r=['ml_cv', 'ml_raw', 'vecs'], w=['mqT'])
                S.dve(lambda e: e.tensor_copy(out=kT[0:64, :], in_=qT[64:128, :]), r=['mqT'], w=['mkT'])
                for t in range(18):
                    g = gidx(t * 128)
                    MM(psb[6][:, 0:64], [(kT[0:64, t * 128:(t + 1) * 128], consts_b[0:64, C_IDENT, 0:64])], ['mkT', 'consts_b'], [('ps', 6)])
                    for dr in range(2):
                        S.act(lambda e, t=t, dr=dr, h=h: e.activation(out=khat[dr][:, t, :], in_=psb[6][:, 0:64], func=AF.Identity,
                                                                     scale=WW[:, t, 8 * dr + h:8 * dr + h + 1]),
                              r=[('ps', 6), 'WW'], w=[('khat', dr, t)])
                    MM(psb[0][:, 0:128], [(hT[:, k, t * 128:(t + 1) * 128], wvh[:, k, :]) for k in range(KC)], ['wvh'] + allh(g), [('ps', 0)])
                    S.dve(lambda e, t=t: e.tensor_copy(out=vp[:, t, 0:128], in_=psb[0][:, 0:128]), r=[('ps', 0)], w=[('vp', t)])
                for (s, n, mc) in LAT:
                    g = gidx(s)
                    MM(psb[7][:, 0:n], [(woh[:, k, :], hT[:, k, s:s + n]) for k in range(KC)], ['woh'] + allh(g), [('ps', 7)])
                    S.act(lambda e, n=n: e.activation(out=esc[:, 0:n], in_=psb[7][:, 0:n], func=AF.Exp, scale=-1.0), r=[('ps', 7)], w=['ml_esc'])
                    S.pool(lambda e, n=n: e.tensor_scalar(out=esc[:, 0:n], in0=esc[:, 0:n], scalar1=1.0, scalar2=None, op0=ALU.add), r=['ml_esc'], w=['ml_esc'])
                    S.dve(lambda e, n=n: e.reciprocal(out=esc[:, 0:n], in_=esc[:, 0:n]), r=['ml_esc'], w=['ml_esc'])
                    S.act(lambda e, s=s, n=n: e.copy(out=sigT[:, s - TC:s - TC + n], in_=esc[:, 0:n]), r=['ml_esc'], w=[('sigT', g)])
                order = [[0, 1] + list(range(2, 18)), [1, 0] + list(range(17, 1, -1))]
                for step in range(18):
                    for dr in range(2):
                        t = order[dr][step]
                        c = 8 * dr + h
                        last = (step == 17)
                        if t >= 2:
                            tl = t - 2
                            b1 = 1 + dr
                            MM(psb[b1][:, 0:128], [(kT[0:64, t * 128:(t + 1) * 128], qT[0:64, t * 128:(t + 1) * 128])], ['mkT', 'mqT'], [('ps', b1)])
                            S.dve(lambda e, t=t, c=c, dr=dr, b1=b1: e.scalar_tensor_tensor(
                                out=Sm[dr], in0=psb[b1][:, 0:128], scalar=BETA[:, t, c:c + 1],
                                in1=consts_b[:, C_LE if dr == 0 else C_GE, :], op0=ALU.mult, op1=ALU.mult),
                                r=[('ps', b1), 'BETA', 'consts_b'], w=[('Sm', dr)])
                            b2 = 3 + dr
                            MM(psb[b2][:, 0:129], [(qT[0:64, t * 128:(t + 1) * 128], Cbf[dr][0:64, 0:129]), (Sm[dr], vp[:, t, 0:129])],
                               ['mqT', ('Cbf', dr), ('Sm', dr), ('vp', t)], [('ps', b2)])
                            S.dve(lambda e, tl=tl, dr=dr, b2=b2: e.tensor_copy(out=numb[dr][:, tl, 0:129], in_=psb[b2][:, 0:129]),
                                  r=[('ps', b2)], w=[('numb', dr)])
                        if not last:
                            b3 = 5 + dr
                            MM(psb[b3][0:64, 0:129], [(khat[dr][:, t, :], vp[:, t, 0:129])], [('khat', dr, t), ('vp', t)], [('ps', b3)])
                            if step == 0:
                                S.dve(lambda e, dr=dr, b3=b3: e.tensor_copy(out=C32[dr][0:64, 0:129], in_=psb[b3][0:64, 0:129]),
                                      r=[('ps', b3)], w=[('C32', dr)])
                            else:
                                S.dve(lambda e, t=t, c=c, dr=dr, b3=b3: e.scalar_tensor_tensor(
                                    out=C32[dr][0:64, 0:129], in0=C32[dr][0:64, 0:129], scalar=DEC[0:64, t, c:c + 1],
                                    in1=psb[b3][0:64, 0:129], op0=ALU.mult, op1=ALU.add),
                                    r=[('ps', b3), ('C32', dr), 'DEC'], w=[('C32', dr)])
                            S.act(lambda e, dr=dr: e.copy(out=Cbf[dr][0:64, 0:129], in_=C32[dr][0:64, 0:129]), r=[('C32', dr)], w=[('Cbf', dr)])
                S.fence()
                for dr in range(2):
                    c = 8 * dr + h
                    S.act(lambda e, dr=dr: e.activation(out=rcb[dr], in_=numb[dr][:, :, 128], func=AF.Abs), r=[('numb', dr)], w=[('rcb', dr)])
                    S.dve(lambda e, dr=dr, c=c: e.tensor_tensor(out=rcb[dr], in0=rcb[dr], in1=EPSI[:, 2:18, c], op=ALU.max),
                          r=[('rcb', dr), 'EPSI'], w=[('rcb', dr)])
                    S.dve(lambda e, dr=dr: e.reciprocal(out=rcb[dr], in_=rcb[dr]), r=[('rcb', dr)], w=[('rcb', dr)])
                    S.dve(lambda e, dr=dr: e.tensor_tensor(out=numb[dr][:, :, 0:128], in0=numb[dr][:, :, 0:128],
                                                          in1=rcb[dr].unsqueeze(2).broadcast_to([128, 16, 128]), op=ALU.mult),
                          r=[('rcb', dr), ('numb', dr)], w=[('numb', dr)])
                S.dve(lambda e: e.tensor_tensor(out=hacc, in0=hacc, in1=numb[1][:, :, 0:128], op=ALU.add),
                      r=[('numb', 0), ('numb', 1)], w=[('numb', 0)])
                allacc = [('numb', 0)]
                sqj = raw[:, 0:2048].rearrange("p (t n) -> p t n", t=16)
                S.dve(lambda e: e.tensor_tensor(out=sqj, in0=hacc, in1=hacc, op=ALU.mult), r=allacc, w=['ml_raw'])
                S.dve(lambda e: e.tensor_reduce(out=ss, in_=sqj, axis=AX.X, op=ALU.add), r=['ml_raw'], w=['ml_ss'])
                S.act(lambda e: e.activation(out=ss, in_=ss, func=AF.Ln, scale=1.0 / 128, bias=epsT), r=['ml_ss', 'epsT'], w=['ml_ss'])
                S.act(lambda e: e.activation(out=ss, in_=ss, func=AF.Exp, scale=-0.5), r=['ml_ss'], w=['ml_ss'])
                S.dve(lambda e: e.tensor_tensor(out=hn, in0=hacc, in1=ss.unsqueeze(2).broadcast_to([128, 16, 128]), op=ALU.mult),
                      r=allacc + ['ml_ss'], w=['ml_hn'])
                for q4 in range(4):
                    def tr(e, q4=q4):
                        ins = None
                        for j in range(4):
                            ins = e.matmul(psb[6][:, j * 128:(j + 1) * 128], lhsT=hn[:, q4 * 4 + j, :], rhs=consts_b[:, C_IDENT, :], start=True, stop=True)
                        return ins
                    S.pe(tr, r=['ml_hn', 'consts_b'], w=[('ps', 6)])
                    S.dve(lambda e, q4=q4, h=h: e.scalar_tensor_tensor(
                        out=gT[:, q4 * 512:(q4 + 1) * 512], in0=psb[6][:, 0:512], scalar=V('mog', 1, h), in1=sigT[:, q4 * 512:(q4 + 1) * 512],
                        op0=ALU.mult, op1=ALU.mult), r=[('ps', 6), 'vecs', ('sigT', 1 + q4)], w=[('gT', q4)])
                for q4, (s, n, mc) in enumerate(LAT):
                    g = gidx(s)
                    for d in range(KC):
                        bank = 7 if d % 2 == 0 else 0
                        MM(psb[bank][:, 0:n], [(wout[:, d * 128:(d + 1) * 128], gT[:, q4 * 512:(q4 + 1) * 512])], ['wout', ('gT', q4)], [('ps', bank)])
                        S.dve(lambda e, d=d, s=s, n=n, mc=mc, bank=bank: e.scalar_tensor_tensor(
                            out=xT[:, d, s:s + n], in0=psb[bank][:, 0:n], scalar=modv(l, 2, d, mc), in1=xT[:, d, s:s + n],
                            op0=ALU.mult, op1=ALU.add), r=[('ps', bank), ('mod', l), ('x', d, g)], w=[('x', d, g)])
            S.fence()

        emit_mod(0, B0)
        S.fence()
        nlayers = flags.get('nlayers', 2)
        if flags.get('mix0', True):
            try:
                emit_attn(0)
            except StopEmit:
                flags['nlayers'] = 0
        nlayers = flags.get('nlayers', 2)
        if flags.get('nlayers', 2) > 0:
            Bf = Bump(PH, ARENA_BYTES - 16384)
            side = None
            if nlayers > 1:
                Bt = Bump(ARENA_BYTES - 16384)
                stage1 = [Bt.take(KC * 256, F32).rearrange("p (k n) -> p k n", k=KC) for _ in range(2)]
                side = mod_gen(1, stage1, 256)
            emit_norm(0, 1, TGS, Bf)
            emit_ffn(0, TGS, Bf, side)
            if side is not None:
                for _ in side:
                    pass
            S.fence()
        if nlayers > 1:
            if flags.get('mix1', True):
                emit_mlstm(1)
            Bf = Bump(PH)
            emit_norm(1, 1, LAT, Bf)
            emit_ffn(1, LAT, Bf)
            S.fence()
        for k in range(KC):
            S.dma('sp', lambda e, k=k: e.dma_start(out=yT_out[:, k, :], in_=xT[:, k, TC:T]),
                  r=[('x', k, g) for g in range(5)], w=[('y', k)])
        S.add('sp', lambda e: e.nop(), r=[('y', k) for k in range(KC)])
        S.emit()
    return nc, S.stats


_CACHE = {}


def _prep_shared(inp):
    sh = {}
    vec, voff = _build_vecs(inp)
    sh['vecs'] = vec
    sh['consts'] = _consts()
    sh['cosT'], sh['sinT'] = _rope_tables()
    for l in range(2):
        sh['ada_w%d' % l] = _pk(np.asarray(inp['ada_w'][l], np.float32))
        pk = _pk(np.asarray(inp['ffn_w_up'][l], np.float32))
        sh['w_up%d' % l] = np.ascontiguousarray(np.stack(
            [np.concatenate([pk[:, :, f * 128:(f + 1) * 128], pk[:, :, DFF + f * 128:DFF + (f + 1) * 128]], axis=2)
             for f in range(NF)], axis=0))
        pd = _pk(np.asarray(inp['ffn_w_down'][l], np.float32))
        sh['w_dn%d' % l] = np.ascontiguousarray(np.stack([pd[:, :, d * 128:(d + 1) * 128] for d in range(KC)], axis=0))
    sh['att_w_in'] = _pk(np.asarray(inp['att_w_in'][0], np.float32))
    sh['mla_w_qb'] = _pk(np.asarray(inp['mla_w_qb'][0], np.float32))
    sh['mla_w_kvb'] = _pk(np.asarray(inp['mla_w_kvb'][0], np.float32))
    sh['att_w_out'] = _pk(np.asarray(inp['att_w_out'][0], np.float32))
    sh['ml_w_in'] = _pk(np.asarray(inp['ml_w_in'][0], np.float32))
    sh['ml_w_out'] = _pk(np.asarray(inp['ml_w_out'][0], np.float32))
    return sh, voff


def run(inputs, flags=None, n_cores=8):
    flags = flags or {}
    inp = {k: np.asarray(v) for k, v in inputs.items()}
    sh, voff = _prep_shared(inp)
    nv = sh['vecs'].shape[1]
    nc, stats = build_program(voff, nv, flags)
    in_maps = []
    for b in range(n_cores):
        xcat = np.concatenate([inp['ctx'][b], inp['x'][b]], axis=0).astype(np.float32)
        m = dict(sh)
        m['xT_in'] = _pk(np.ascontiguousarray(xcat.T))
        m['c_in'] = _pk(np.ascontiguousarray(np.stack([inp['c'][b], inp['c_ctx']], axis=1).astype(np.float32)))
        in_maps.append(m)
    res = run_bass_kernel_spmd(nc, in_maps, core_ids=list(range(n_cores)))
    outs = []
    for b in range(n_cores):
        yT = np.asarray(res.results[b]['yT'])
        outs.append(np.ascontiguousarray(yT.transpose(1, 0, 2).reshape(D, TL).T))
    return np.stack(outs, axis=0).astype(np.float32), stats


def kernel(**inputs):
    out, _ = run(inputs)
    return out
```
